# Optimizing a Trainium2 kernel written in Bass

```python
import math
import jax, jax.numpy as jnp
from jax import lax
import numpy as np

D_MODEL = 1024
BATCH = 1
SEQ = 16384
DEPTH = 2

D_RWKV = D_MODEL // 2
RWKV_HEAD = 64
RWKV_HEADS = D_RWKV // RWKV_HEAD
D_CONV = D_MODEL - D_RWKV
CONV_WIDTH = 3
LORA_W = 64
LORA_A = 64
LORA_G = 128
D_IN_A = 3 * D_RWKV + LORA_W + LORA_A + LORA_G
D_IN_B = 3 * D_CONV
D_IN = D_IN_A + D_IN_B
GN_EPS = 64e-5
SSM_GROUP = 16
SSM_GROUPS = D_MODEL // SSM_GROUP
SSM_STATE = 64
SSM_CHUNK = 128
D_FF = 4 * D_MODEL
D_PLE = 256
RMS_EPS = 1e-6

kernel_name = 'hybrid_rwkv7_shortconv_s5_block'


def rms_norm(x, g):
    xf = x.astype(jnp.float32)
    y = xf * lax.rsqrt(jnp.mean(xf * xf, axis=-1, keepdims=True) + RMS_EPS)
    return (y * g.astype(jnp.float32)).astype(x.dtype)


def shift_prev(x, n=1):
    return jnp.pad(x, ((0, 0), (n, 0), (0, 0)))[:, :x.shape[1]]


def rwkv7_recurrence(r, w, k, v, a, b):
    bsz, _, heads, n = r.shape

    def step(s, inp):
        r_t, w_t, k_t, v_t, a_t, b_t = inp
        sa = jnp.einsum('bhij,bhj->bhi', s, a_t)
        s = s * w_t[:, :, None, :] + sa[..., :, None] * b_t[..., None, :] + v_t[..., :, None] * k_t[..., None, :]
        return s, jnp.einsum('bhij,bhj->bhi', s, r_t)

    xs = tuple(jnp.moveaxis(t, 1, 0) for t in (r, w, k, v, a, b))
    s0 = jnp.zeros((bsz, heads, n, n), jnp.float32)
    _, y = lax.scan(step, s0, xs)
    return jnp.moveaxis(y, 0, 1)


def rwkv_conv_mixer(hn, w_in, shift_mu, w0, w_lora_up, a0, a_lora_up, g_lora_up,
                    k_k, k_a, r_k, ln_w, ln_b, conv_w, w_out):
    bsz, t, _ = hn.shape
    f32 = jnp.float32
    z = hn @ w_in
    za, zb = z[..., :D_IN_A], z[..., D_IN_A:]
    za = za + shift_mu * (shift_prev(za) - za)
    r, k, v, xw, xa, xg = jnp.split(
        za, [D_RWKV, 2 * D_RWKV, 3 * D_RWKV, 3 * D_RWKV + LORA_W, 3 * D_RWKV + LORA_W + LORA_A], axis=-1)
    w_log = -jax.nn.softplus(-(w0 + jnp.tanh(xw) @ w_lora_up)) - 0.5
    decay = jnp.exp(-jnp.exp(w_log.astype(f32)))
    a = jax.nn.sigmoid(a0 + xa @ a_lora_up)
    g = jax.nn.sigmoid(xg) @ g_lora_up

    def heads(u):
        return u.reshape(bsz, t, RWKV_HEADS, RWKV_HEAD).astype(f32)

    kk = heads(k * k_k)
    kk = kk / jnp.maximum(jnp.sqrt(jnp.sum(kk * kk, axis=-1, keepdims=True)), 1e-12)
    k = k * (1.0 + (a - 1.0) * k_a)
    r_h, k_h, v_h, a_h = heads(r), heads(k), heads(v), heads(a)
    y = rwkv7_recurrence(r_h, heads(decay), k_h, v_h, -kk, kk * a_h)
    mu = jnp.mean(y, axis=-1, keepdims=True)
    var = jnp.mean(jnp.square(y - mu), axis=-1, keepdims=True)
    y = (y - mu) * lax.rsqrt(var + GN_EPS)
    y = y * ln_w.reshape(RWKV_HEADS, RWKV_HEAD).astype(f32) + ln_b.reshape(RWKV_HEADS, RWKV_HEAD).astype(f32)
    bonus = jnp.sum(r_h * k_h * r_k.astype(f32), axis=-1, keepdims=True) * v_h
    y_a = ((y + bonus).reshape(bsz, t, D_RWKV) * g.astype(f32)).astype(hn.dtype)
    b_gate, c_gate, xin = jnp.split(zb, 3, axis=-1)
    u = c_gate * xin
    conv = conv_w[0] * u + conv_w[1] * shift_prev(u, 1) + conv_w[2] * shift_prev(u, 2)
    y_b = b_gate * conv
    return jnp.concatenate([y_a, y_b], axis=-1) @ w_out


def _affine_combine(e1, e2):
    a1, b1 = e1
    a2, b2 = e2
    return a2 * a1, a2 * b1 + b2


def s5_mixer(hn, lambda_re, lambda_im, log_step, b_re, b_im, c_re, c_im, d_skip, glu_w1, glu_w2):
    bsz, t, _ = hn.shape
    f32, c64 = jnp.float32, jnp.complex64
    u = hn.astype(f32).reshape(bsz, t, SSM_GROUPS, SSM_GROUP)
    lam = lax.complex(jnp.minimum(lambda_re.astype(f32), -1e-4), lambda_im.astype(f32))
    step = jnp.exp(log_step.astype(f32))[:, None]
    lam_bar = jnp.exp(lam * step)
    b_bar = ((lam_bar - 1.0) / lam)[..., None] * lax.complex(b_re.astype(f32), b_im.astype(f32))
    c = lax.complex(c_re.astype(f32), c_im.astype(f32))
    n_chunks = t // SSM_CHUNK
    u_chunks = u.reshape(bsz, n_chunks, SSM_CHUNK, SSM_GROUPS, SSM_GROUP).transpose(1, 2, 0, 3, 4)

    def chunk_step(h_prev, u_c):
        bu = jnp.einsum('gpc,lbgc->lbgp', b_bar, u_c.astype(c64))
        a_cum, h_loc = lax.associative_scan(_affine_combine, (jnp.broadcast_to(lam_bar, bu.shape), bu), axis=0)
        states = h_loc + a_cum * h_prev[None]
        y_c = jnp.real(jnp.einsum('gcp,lbgp->lbgc', c, states))
        return states[-1], y_c

    h0 = jnp.zeros((bsz, SSM_GROUPS, SSM_STATE), c64)
    _, y = lax.scan(chunk_step, h0, u_chunks)
    y = y.transpose(2, 0, 1, 3, 4).reshape(bsz, t, SSM_GROUPS, SSM_GROUP)
    y = (y + d_skip.reshape(SSM_GROUPS, SSM_GROUP).astype(f32) * u).reshape(bsz, t, D_MODEL)
    y = jax.nn.gelu(y).astype(hn.dtype)
    return (y @ glu_w1) * jax.nn.sigmoid(y @ glu_w2)


def sqrelu_mlp(hn, ffn_up, ffn_down):
    return jnp.square(jax.nn.relu(hn @ ffn_up)) @ ffn_down


def setup_inputs(seed: int = 0) -> dict:
    key = jax.random.key(seed)
    ks = iter(jax.random.split(key, 64))

    def nrm(shape, scale):
        return jax.random.normal(next(ks), shape, jnp.float32) * scale

    def gain(n):
        return 1.0 + nrm((n,), 0.02)

    d = D_MODEL
    ratio = jnp.arange(D_RWKV, dtype=jnp.float32) / (D_RWKV - 1)
    inp = {}
    inp['x'] = nrm((BATCH, SEQ, d), 1.0)
    inp['p'] = nrm((DEPTH, BATCH, SEQ, D_PLE), 1.0)
    inp['l0_norm_mix'] = gain(d)
    inp['l0_w_in'] = nrm((d, D_IN), d ** -0.5)
    inp['l0_shift_mu'] = jax.random.uniform(next(ks), (D_IN_A,), jnp.float32)
    inp['l0_w0'] = -6.5 + 5.0 * ratio ** 0.85 + nrm((D_RWKV,), 0.1)
    inp['l0_w_lora_up'] = nrm((LORA_W, D_RWKV), 0.1 * LORA_W ** -0.5)
    inp['l0_a0'] = nrm((D_RWKV,), 0.1)
    inp['l0_a_lora_up'] = nrm((LORA_A, D_RWKV), 0.1 * LORA_A ** -0.5)
    inp['l0_g_lora_up'] = nrm((LORA_G, D_RWKV), LORA_G ** -0.5)
    inp['l0_k_k'] = 0.85 + nrm((D_RWKV,), 0.02)
    inp['l0_k_a'] = 1.0 + nrm((D_RWKV,), 0.02)
    inp['l0_r_k'] = nrm((RWKV_HEADS, RWKV_HEAD), 0.1)
    inp['l0_ln_w'] = gain(D_RWKV)
    inp['l0_ln_b'] = nrm((D_RWKV,), 0.02)
    inp['l0_conv_w'] = nrm((CONV_WIDTH, D_CONV), CONV_WIDTH ** -0.5)
    inp['l0_w_out'] = nrm((d, d), d ** -0.5)
    inp['l0_norm_ffn'] = gain(d)
    inp['l0_ffn_up'] = nrm((d, D_FF), d ** -0.5)
    inp['l0_ffn_down'] = nrm((D_FF, d), D_FF ** -0.5)
    inp['l0_norm_ple'] = gain(d)
    inp['l0_ple_gate'] = nrm((d, d), d ** -0.5)
    inp['l0_ple_proj'] = nrm((D_PLE, d), D_PLE ** -0.5)
    inp['l1_norm_mix'] = gain(d)
    inp['l1_lambda_re'] = -0.5 + nrm((SSM_GROUPS, SSM_STATE), 0.01)
    inp['l1_lambda_im'] = math.pi * jnp.arange(SSM_STATE, dtype=jnp.float32)[None, :] + nrm((SSM_GROUPS, SSM_STATE), 0.01)
    inp['l1_log_step'] = jax.random.uniform(next(ks), (SSM_GROUPS,), jnp.float32, math.log(1e-3), math.log(1e-1))
    inp['l1_b_re'] = nrm((SSM_GROUPS, SSM_STATE, SSM_GROUP), (2 * SSM_GROUP) ** -0.5)
    inp['l1_b_im'] = nrm((SSM_GROUPS, SSM_STATE, SSM_GROUP), (2 * SSM_GROUP) ** -0.5)
    inp['l1_c_re'] = nrm((SSM_GROUPS, SSM_GROUP, SSM_STATE), SSM_STATE ** -0.5)
    inp['l1_c_im'] = nrm((SSM_GROUPS, SSM_GROUP, SSM_STATE), SSM_STATE ** -0.5)
    inp['l1_d_skip'] = nrm((d,), 1.0)
    inp['l1_glu_w1'] = nrm((d, d), d ** -0.5)
    inp['l1_glu_w2'] = nrm((d, d), d ** -0.5)
    inp['l1_norm_ffn'] = gain(d)
    inp['l1_ffn_up'] = nrm((d, D_FF), d ** -0.5)
    inp['l1_ffn_down'] = nrm((D_FF, d), D_FF ** -0.5)
    inp['l1_norm_ple'] = gain(d)
    inp['l1_ple_gate'] = nrm((d, d), d ** -0.5)
    inp['l1_ple_proj'] = nrm((D_PLE, d), D_PLE ** -0.5)
    inp['norm_final'] = gain(d)
    return inp


def reference(x, p,
              l0_norm_mix, l0_w_in, l0_shift_mu, l0_w0, l0_w_lora_up, l0_a0, l0_a_lora_up, l0_g_lora_up,
              l0_k_k, l0_k_a, l0_r_k, l0_ln_w, l0_ln_b, l0_conv_w, l0_w_out,
              l0_norm_ffn, l0_ffn_up, l0_ffn_down, l0_norm_ple, l0_ple_gate, l0_ple_proj,
              l1_norm_mix, l1_lambda_re, l1_lambda_im, l1_log_step, l1_b_re, l1_b_im, l1_c_re, l1_c_im,
              l1_d_skip, l1_glu_w1, l1_glu_w2,
              l1_norm_ffn, l1_ffn_up, l1_ffn_down, l1_norm_ple, l1_ple_gate, l1_ple_proj,
              norm_final):
    mix_fns = [rwkv_conv_mixer, s5_mixer]
    mix_params = [
        (l0_w_in, l0_shift_mu, l0_w0, l0_w_lora_up, l0_a0, l0_a_lora_up, l0_g_lora_up,
         l0_k_k, l0_k_a, l0_r_k, l0_ln_w, l0_ln_b, l0_conv_w, l0_w_out),
        (l1_lambda_re, l1_lambda_im, l1_log_step, l1_b_re, l1_b_im, l1_c_re, l1_c_im,
         l1_d_skip, l1_glu_w1, l1_glu_w2),
    ]
    norm_mix = [l0_norm_mix, l1_norm_mix]
    norm_ffn = [l0_norm_ffn, l1_norm_ffn]
    ffn = [(l0_ffn_up, l0_ffn_down), (l1_ffn_up, l1_ffn_down)]
    norm_ple = [l0_norm_ple, l1_norm_ple]
    ple = [(l0_ple_gate, l0_ple_proj), (l1_ple_gate, l1_ple_proj)]

    h = x
    for i in range(DEPTH):
        h = h + mix_fns[i % 2](rms_norm(h, norm_mix[i]), *mix_params[i])
        h = h + sqrelu_mlp(rms_norm(h, norm_ffn[i]), *ffn[i])
        gate = jax.nn.sigmoid(rms_norm(h, norm_ple[i]) @ ple[i][0])
        h = h + (p[i] @ ple[i][1]) * gate
    return rms_norm(h, norm_final)
```

```python
import math
from contextlib import ExitStack

import numpy as np
import ml_dtypes

import concourse.bass as bass
import concourse.mybir as mybir
from concourse.bass_utils import run_bass_kernel_spmd

F32 = mybir.dt.float32
BF16 = mybir.dt.bfloat16
AF = mybir.ActivationFunctionType
ALU = mybir.AluOpType

NCORES = 8
SEQ = 16384
D = 1024
TPC = SEQ // NCORES
C0 = math.exp(-0.5)
RMS_EPS = 1e-6
GN_EPS = 64e-5


class V:
    __slots__ = ("ap", "res")

    def __init__(self, ap, res):
        self.ap = ap
        self.res = res


class T:
    def __init__(self, handle, name):
        self.h = handle
        self.name = name

    def __getitem__(self, key):
        return V(self.h[key], self.name)

    def sub(self, subkey, key):
        return V(self.h[key], (self.name, subkey))


class KB:
    def __init__(self):
        self.nc = bass.Bass("TRN2", target_bir_lowering=False)
        self.es = ExitStack()
        nc = self.nc
        self.E = {"pe": nc.tensor, "act": nc.scalar, "dve": nc.vector, "pool": nc.gpsimd, "sp": nc.sync}
        self.csem = {}
        self.cnt = {}
        for k in self.E:
            self.csem[k] = self.es.enter_context(nc.semaphore("c_" + k))
            self.cnt[k] = 0
        self.dsem = {}
        self.dcnt = {}
        self.waited = {k: {} for k in self.E}
        self.lastw = {}
        self.rd = {}
        self.out_streams = set()
        self.ntile = 0
        self.psum_names = set()

    def sb(self, shape, dt, name=None):
        self.ntile += 1
        name = name or f"t{self.ntile}"
        h = self.es.enter_context(self.nc.sbuf_tensor("s_" + name, list(shape), dt))
        return T(h, name)

    def ps(self, shape, dt=F32, name=None):
        self.ntile += 1
        name = name or f"p{self.ntile}"
        h = self.es.enter_context(self.nc.psum_tensor("ps_" + name, list(shape), dt))
        self.psum_names.add(name)
        return T(h, name)

    def dram(self, name, shape, dt, kind):
        return self.nc.dram_tensor(name, list(shape), dt, kind=kind).ap()

    def _wait(self, e, ev):
        if ev[0] == "c":
            _, k, val = ev
            if k == e and e == "pe":
                return
            sem = self.csem[k]
            name = "c_" + k
        else:
            _, s = ev
            sem = self.dsem[s]
            val = self.dcnt[s]
            name = "d_" + s
        if self.waited[e].get(name, 0) >= val:
            return
        self.E[e].wait_ge(sem, val)
        self.waited[e][name] = val

    def _deps(self, e, reads, writes):
        for r in reads:
            ev = self.lastw.get(r)
            if ev:
                self._wait(e, ev)
            base = r[0] if isinstance(r, tuple) else r
            if base in self.psum_names:
                for ev in self.rd.get(r, {}).values():
                    if ev[0] == "c" and ev[1] == e:
                        continue
                    self._wait(e, ev)
        for w in writes:
            ev = self.lastw.get(w)
            if ev:
                self._wait(e, ev)
            for ev in self.rd.get(w, {}).values():
                self._wait(e, ev)

    def _commit(self, ev, key, reads, writes):
        for w in writes:
            self.lastw[w] = ev
            self.rd[w] = {}
        for r in reads:
            if r in writes:
                continue
            self.rd.setdefault(r, {})[key] = ev

    def op(self, e, emit, reads, writes):
        reads = [v.res for v in reads if isinstance(v, V)]
        writes = [v.res for v in writes if isinstance(v, V)]
        self._deps(e, reads, writes)
        ins = emit(self.E[e])
        self.cnt[e] += 1
        ins.then_inc(self.csem[e], 1)
        self._commit(("c", e, self.cnt[e]), "c_" + e, reads, writes)

    def dma(self, q, stream, out, in_, is_output=False):
        if stream not in self.dsem:
            self.dsem[stream] = self.es.enter_context(self.nc.semaphore("d_" + stream))
            self.dcnt[stream] = 0
        reads = [in_.res] if isinstance(in_, V) else []
        writes = [out.res] if isinstance(out, V) else []
        self._deps(q, reads, writes)
        o = out.ap if isinstance(out, V) else out
        i = in_.ap if isinstance(in_, V) else in_
        self.E[q].dma_start(out=o, in_=i).then_inc(self.dsem[stream], 16)
        self.dcnt[stream] += 16
        self._commit(("d", stream), "d_" + stream, reads, writes)
        if is_output:
            self.out_streams.add(stream)

    def finish(self):
        for s in sorted(self.out_streams):
            self.E["sp"].wait_ge(self.dsem[s], self.dcnt[s])
        self.es.close()

    @staticmethod
    def _a(x):
        return x.ap if isinstance(x, V) else x

    def mm(self, out, lhsT, rhs, start=True, stop=True):
        self.op("pe", lambda e: e.matmul(out.ap, lhsT=lhsT.ap, rhs=rhs.ap, start=start, stop=stop),
                [lhsT, rhs], [out])

    def act(self, out, in_, func, bias=0.0, scale=1.0, e="act"):
        a = self._a
        self.op(e, lambda en: en.activation(out=out.ap, in_=in_.ap, func=func, bias=a(bias), scale=a(scale)),
                [in_, bias, scale], [out])

    def tt(self, e, out, in0, in1, op):
        self.op(e, lambda en: en.tensor_tensor(out=out.ap, in0=in0.ap, in1=in1.ap, op=op), [in0, in1], [out])

    def ts(self, e, out, in0, s1, op0, s2=None, op1=None):
        a = self._a
        if op1 is None:
            self.op(e, lambda en: en.tensor_scalar(out=out.ap, in0=in0.ap, scalar1=a(s1), scalar2=None, op0=op0),
                    [in0, s1], [out])
        else:
            self.op(e, lambda en: en.tensor_scalar(out=out.ap, in0=in0.ap, scalar1=a(s1), scalar2=a(s2),
                                                   op0=op0, op1=op1), [in0, s1, s2], [out])

    def stt(self, out, in0, scalar, in1, op0, op1):
        a = self._a
        self.op("dve", lambda en: en.scalar_tensor_tensor(out=out.ap, in0=in0.ap, scalar=a(scalar), in1=in1.ap,
                                                         op0=op0, op1=op1), [in0, scalar, in1], [out])

    def copy(self, e, out, in_):
        if e == "act":
            self.op(e, lambda en: en.copy(out=out.ap, in_=in_.ap), [in_], [out])
        else:
            self.op(e, lambda en: en.tensor_copy(out=out.ap, in_=in_.ap), [in_], [out])

    def scan(self, out, d0, d1, initial, op0, op1):
        a = self._a
        self.op("dve", lambda en: en.tensor_tensor_scan(out=out.ap, data0=d0.ap, data1=d1.ap, initial=a(initial),
                                                       op0=op0, op1=op1), [d0, d1, initial], [out])


def _run(kb, in_maps):
    res = run_bass_kernel_spmd(kb.nc, in_maps, core_ids=list(range(NCORES)))
    return res.results


BLK = 256
NBLK = TPC // BLK
BW = BLK + 2
L1_VEC = dict(g=0, mu=8, w0=22, a0=26, kk=30, ka=34, rk=38, cw0=42, cw1=46, cw2=50)
L1_NV = 54
L1_OUTS = ["o_at", "o_bt", "o_kt", "o_rt", "o_bh", "o_kh", "o_v", "o_g", "o_bonus"]


def build_l1():
    kb = KB()
    xT = kb.dram("xT", [128, 8, TPC + 2], F32, "ExternalInput")
    w_in = kb.dram("w_in", [26, 128, 8 * 128], F32, "ExternalInput")
    vec = kb.dram("vec", [128, L1_NV], F32, "ExternalInput")
    wa_up = kb.dram("wa_up", [128, 512], F32, "ExternalInput")
    g_up = kb.dram("g_up", [128, 512], F32, "ExternalInput")
    cst = kb.dram("cst", [128, 128 + BLK], F32, "ExternalInput")
    outs = {n: kb.dram(n, [128, 4, TPC], F32, "ExternalOutput") for n in L1_OUTS}
    o_gc = kb.dram("o_gc", [128, 4, TPC // 64], F32, "ExternalOutput")
    o_yb = kb.dram("o_yb", [128, 4, TPC], F32, "ExternalOutput")

    vec_t = kb.sb([128, L1_NV], F32, "vec")
    kb.dma("sp", "ld_c", vec_t[:], vec)
    cst_t = kb.sb([128, 128 + BLK], F32, "cst")
    kb.dma("sp", "ld_c", cst_t[:], cst)
    omv = kb.sb([128, 18], F32, "omv")
    kb.ts("dve", omv[:, 0:14], vec_t[:, 8:22], -1.0, ALU.mult, 1.0, ALU.add)
    kb.ts("dve", omv[:, 14:18], vec_t[:, 34:38], -1.0, ALU.mult, 1.0, ALU.add)
    ones_bf = kb.sb([128, 128], BF16, "ones_bf")
    epsc = kb.sb([128, 1], F32, "epsc")
    kb.op("pool", lambda e: e.memset(epsc.h[:], RMS_EPS), [], [epsc[:]])
    kb.op("pool", lambda e: e.memset(ones_bf.h[:], 1.0), [], [ones_bf[:]])

    def vcol(name, i=0):
        c = L1_VEC[name] + i
        return vec_t[:, c:c + 1]

    wstage = [kb.sb([128, 1024], F32, f"wst{i}") for i in range(2)]
    w_bf = kb.sb([128, 26, 8, 128], BF16, "w_bf")
    for j in range(26):
        st = wstage[j % 2]
        kb.dma("sp" if j % 2 == 0 else "pool", f"ld_w{j % 2}", st[:], w_in[j])
        eng = ["act", "dve", "pool"][j % 3]
        kb.copy(eng, V(w_bf.h[:, j].rearrange("p a b -> p (a b)"), ("w_bf", j)), st[:])
    up_st = kb.sb([128, 1024], F32, "up_st")
    kb.dma("sp", "ld_c", up_st[:, 0:512], wa_up)
    kb.dma("sp", "ld_c", up_st[:, 512:1024], g_up)
    up_bf = kb.sb([128, 1024], BF16, "up_bf")
    kb.copy("dve", up_bf[:], up_st[:])

    xb = [kb.sb([128, 8, BW], F32, f"xb{i}") for i in range(2)]
    sqb = kb.sb([128, 8, BW], BF16, "sqb")
    hn = kb.sb([128, 8, BW], BF16, "hn")
    rstd = kb.sb([128, BW], F32, "rstd")
    psb = kb.ps([128, 8, 512], F32, "psb")
    pbank = [0]

    def bank():
        b = pbank[0] % 8
        pbank[0] += 1
        return b

    def P(b, n, lo=0):
        return V(psb.h[:, b, lo:lo + n], ("psb", b))

    lo_bf = kb.sb([128, BLK], BF16, "lo_bf")
    sg_bf = kb.sb([128, BLK], BF16, "sg_bf")
    tmp = [kb.sb([128, BW], F32, f"tmp{i}") for i in range(3)]
    NW = 22
    wk = [[kb.sb([128, BLK], F32, f"wk{s}_{i}") for i in range(NW)] for s in range(2)]

    def zmat(j, bnk):
        for kt in range(8):
            kb.mm(P(bnk, BW), V(w_bf.h[:, j, kt, :], ("w_bf", j)), hn[:, kt, :], start=(kt == 0), stop=(kt == 7))

    def shifted(j, bnk, out, ti):
        t = tmp[ti]
        kb.act(t[:, 0:BLK], P(bnk, BLK, 1), AF.Copy, scale=vcol("mu", j))
        kb.stt(out, P(bnk, BLK, 2), omv[:, j:j + 1], t[:, 0:BLK], ALU.mult, ALU.add)

    odma = [0]

    def store(dst, src):
        q = "sp" if odma[0] % 2 == 0 else "pool"
        kb.dma(q, f"st{odma[0] % 4}", dst, src, is_output=True)
        odma[0] += 1

    for b in range(NBLK):
        c0 = b * BLK
        x_t = xb[b % 2]
        kb.dma("sp", f"ld_x{b % 2}", x_t[:], xT[:, :, c0:c0 + BW])
        for kt in range(8):
            kb.act(sqb[:, kt, :], x_t[:, kt, :], AF.Square)
        bn = bank()
        for kt in range(8):
            kb.mm(P(bn, BW), ones_bf[:], sqb[:, kt, :], start=(kt == 0), stop=(kt == 7))
        kb.act(rstd[:], P(bn, BW), AF.Sqrt, bias=epsc[:, 0:1], scale=1.0 / D)
        kb.op("dve", lambda e: e.reciprocal(out=rstd.h[:], in_=rstd.h[:]), [rstd[:]], [rstd[:]])
        for kt in range(8):
            kb.stt(hn[:, kt, :], x_t[:, kt, :], vcol("g", kt), rstd[:], ALU.mult, ALU.mult)
        bn = bank()
        zmat(12, bn)
        w = wk[b % 2]
        shifted(12, bn, w[0][:], 0)
        kb.act(lo_bf[0:64, :], w[0][0:64, :], AF.Tanh)
        kb.copy("pool", lo_bf[64:128, :], w[0][64:128, :])
        bn = bank()
        zmat(13, bn)
        shifted(13, bn, w[1][:], 1)
        kb.act(sg_bf[:], w[1][:], AF.Sigmoid)
        for p in range(4):
            w = wk[(4 * b + p) % 2]
            r_, k_, v_ = w[0], w[1], w[2]
            for (j, o, ti) in ((p, r_, 0), (4 + p, k_, 1), (8 + p, v_, 2)):
                bn = bank()
                zmat(j, bn)
                shifted(j, bn, o[:], ti)
            sig, a_, g_ = w[3], w[4], w[5]
            bn = bank()
            kb.mm(P(bn, BLK), up_bf[0:64, p * 128:(p + 1) * 128], lo_bf[0:64, :])
            kb.act(sig[:], P(bn, BLK), AF.Sigmoid, bias=vcol("w0", p))
            bn = bank()
            kb.mm(P(bn, BLK), up_bf[64:128, p * 128:(p + 1) * 128], lo_bf[64:128, :])
            kb.act(a_[:], P(bn, BLK), AF.Sigmoid, bias=vcol("a0", p))
            bn = bank()
            kb.mm(P(bn, BLK), up_bf[:, 512 + p * 128:512 + (p + 1) * 128], sg_bf[:])
            kb.copy("act", g_[:], P(bn, BLK))
            sqk, rs, kkn, ka, k2, bv = w[6], w[7], w[8], w[9], w[10], w[11]
            kb.act(sqk[:], k_[:], AF.Square, scale=vcol("kk", p))
            bn = bank()
            kb.mm(P(bn, BLK), cst_t[:, 0:128], sqk[:])
            kb.act(rs[:], P(bn, BLK), AF.Sqrt)
            kb.ts("dve", rs[:], rs[:], 1e-12, ALU.max)
            kb.op("dve", lambda e, rs=rs: e.reciprocal(out=rs.h[:], in_=rs.h[:]), [rs[:]], [rs[:]])
            kb.stt(kkn[:], k_[:], vcol("kk", p), rs[:], ALU.mult, ALU.mult)
            kb.ts("pool", ka[:], a_[:], vcol("ka", p), ALU.mult, omv[:, 14 + p:15 + p], ALU.add)
            kb.tt("pool", k2[:], k_[:], ka[:], ALU.mult)
            kb.tt("pool", bv[:], kkn[:], a_[:], ALU.mult)
            rk, bon = w[12], w[13]
            kb.stt(rk[:], r_[:], vcol("rk", p), k2[:], ALU.mult, ALU.mult)
            bn = bank()
            kb.mm(P(bn, BLK), cst_t[:, 0:128], rk[:])
            kb.tt("dve", bon[:], P(bn, BLK), v_[:], ALU.mult)
            cum, e1, e2, e3, e4, cx = w[14], w[15], w[16], w[17], w[18], w[19]
            kb.scan(cum[:], cst_t[:, 128:128 + BLK], sig[:], 0.0, ALU.mult, ALU.add)
            kb.act(e1[:], cum[:], AF.Exp, scale=-C0)
            kb.act(e2[:], cum[:], AF.Exp, scale=C0)
            kb.tt("pool", cx[:], cum[:], sig[:], ALU.subtract)
            kb.act(e3[:], cx[:], AF.Exp, scale=-C0)
            nch = BLK // 64
            cum3 = cum.h[:].rearrange("p (c t) -> p c t", t=64)
            cumC = cum3[:, :, 63:64].to_broadcast([128, nch, 64])
            kb.op("pool", lambda e: e.tensor_tensor(out=cx.h[:].rearrange("p (c t) -> p c t", t=64), in0=cumC,
                                                    in1=cum3, op=ALU.subtract), [cum[:], e3[:]], [cx[:]])
            kb.act(e4[:], cx[:], AF.Exp, scale=-C0)
            gc = w[20]
            kb.copy("pool", V(gc.h[:, 0:nch], gc.name), V(e1.h[:].rearrange("p (c t) -> p c t", t=64)[:, :, 63], e1.name))
            at, bt, kt_, rt, bh, kh = w[21], sqk, rs, ka, cx, cum
            kb.stt(at[:], kkn[:], -1.0, e3[:], ALU.mult, ALU.mult)
            kb.tt("dve", bt[:], bv[:], e2[:], ALU.mult)
            kb.tt("pool", kt_[:], k2[:], e2[:], ALU.mult)
            kb.tt("dve", rt[:], r_[:], e1[:], ALU.mult)
            kb.tt("pool", bh[:], bv[:], e4[:], ALU.mult)
            kb.tt("dve", kh[:], k2[:], e4[:], ALU.mult)
            for nm, tl in (("o_at", at), ("o_bt", bt), ("o_kt", kt_), ("o_rt", rt), ("o_bh", bh), ("o_kh", kh),
                           ("o_v", v_), ("o_g", g_), ("o_bonus", bon)):
                store(outs[nm][:, p, c0:c0 + BLK], tl[:])
            store(o_gc[:, p, b * nch:(b + 1) * nch], V(gc.h[:, 0:nch], gc.name))
        for q in range(4):
            w = wk[(4 * b + q) % 2]
            b1, b2, b3 = bank(), bank(), bank()
            zmat(14 + q, b1)
            zmat(18 + q, b2)
            zmat(22 + q, b3)
            cg, u, c1 = tmp[0], tmp[1], tmp[2]
            kb.copy("act", cg[:], P(b2, BW))
            kb.tt("dve", u[:], cg[:], P(b3, BW), ALU.mult)
            kb.ts("pool", c1[:, 0:BLK], u[:, 2:BW], vcol("cw0", q), ALU.mult)
            kb.stt(c1[:, 0:BLK], u[:, 1:BW - 1], vcol("cw1", q), c1[:, 0:BLK], ALU.mult, ALU.add)
            kb.stt(c1[:, 0:BLK], u[:, 0:BLK], vcol("cw2", q), c1[:, 0:BLK], ALU.mult, ALU.add)
            yb = w[0]
            kb.tt("dve", yb[:], P(b1, BLK, 2), c1[:, 0:BLK], ALU.mult)
            store(o_yb[:, q, c0:c0 + BLK], yb[:])
    kb.finish()
    return kb


def prep_l1(inp):
    x = inp["x"][0]
    xp = np.concatenate([np.zeros((2, D), np.float32), x], axis=0)
    w_in = inp["l0_w_in"].reshape(8, 128, 26, 128).transpose(2, 1, 0, 3).reshape(26, 128, 1024)
    w_in = np.ascontiguousarray(w_in)

    def col(v, n):
        return v.reshape(n, 128).T

    vec = np.zeros((128, L1_NV), np.float32)
    vec[:, 0:8] = col(inp["l0_norm_mix"], 8)
    vec[:, 8:22] = col(inp["l0_shift_mu"], 14)
    vec[:, 22:26] = col(inp["l0_w0"], 4)
    vec[:, 26:30] = col(inp["l0_a0"], 4)
    vec[:, 30:34] = col(inp["l0_k_k"], 4)
    vec[:, 34:38] = col(inp["l0_k_a"], 4)
    vec[:, 38:42] = col(inp["l0_r_k"].reshape(-1), 4)
    for i in range(3):
        vec[:, 42 + 4 * i:46 + 4 * i] = col(inp["l0_conv_w"][i], 4)
    wa_up = np.ascontiguousarray(np.concatenate([inp["l0_w_lora_up"], inp["l0_a_lora_up"]], axis=0))
    g_up = np.ascontiguousarray(inp["l0_g_lora_up"])
    cst = np.zeros((128, 128 + BLK), np.float32)
    cst[0:64, 0:64] = 1.0
    cst[64:128, 64:128] = 1.0
    cst[:, 128:] = 1.0
    cst[:, 128::64] = 0.0
    maps = []
    for c in range(NCORES):
        xs = xp[c * TPC:c * TPC + TPC + 2]
        xT = np.ascontiguousarray(xs.T.reshape(8, 128, TPC + 2).transpose(1, 0, 2))
        maps.append(dict(xT=xT, w_in=w_in, vec=vec, wa_up=wa_up, g_up=g_up, cst=cst))
    return maps


CH = 64
NCHK = SEQ // CH
GRP = 8
NGRP = NCHK // GRP


def build_l2(nchk=NCHK, dbg=None):
    kb = KB()
    dbg = dbg or {}
    ngrp = nchk // GRP
    fm_names = ("aT", "bT", "kT", "rT")
    tm_names = ("A", "Vv", "Bh", "Kh")
    fm_in = {n: kb.dram(n, [64, nchk * CH], F32, "ExternalInput") for n in fm_names}
    tm_in = {n: kb.dram(n, [64, nchk, 64], F32, "ExternalInput") for n in tm_names}
    gc_in = kb.dram("gc", [64, nchk], F32, "ExternalInput")
    cst_in = kb.dram("cst2", [64, 4, 64], F32, "ExternalInput")
    y_out = kb.dram("y", [64, nchk, 64], F32, "ExternalOutput")

    cst = kb.sb([64, 4, 64], F32, "cst2s")
    kb.dma("sp", "ld_c", cst[:], cst_in)
    gc = kb.sb([64, nchk], F32, "gcs")
    kb.dma("sp", "ld_c", gc[:], gc_in)

    def cb(i):
        return V(cst.h[:, i:i + 1, :].to_broadcast([64, GRP, 64]), cst.name)

    fm = [{n: kb.sb([64, GRP * CH], F32, f"fm{s}_{n}") for n in fm_names} for s in range(3)]
    tm = [{n: kb.sb([64, GRP, 64], F32, f"tm{s}_{n}") for n in tm_names} for s in range(3)]
    wn = ("M0", "M1", "N0", "N1", "IN", "P0", "P1", "Aak", "Arb", "Ark", "Abar", "W")
    wk = [{n: kb.sb([64, GRP, 64], F32, f"w{s}_{n}") for n in wn} for s in range(2)]
    ytile = [kb.sb([64, GRP, 64], F32, f"yt{s}") for s in range(2)]
    S = [kb.sb([64, 64], F32, f"S{s}") for s in range(2)]
    kb.op("pool", lambda e: e.memset(S[0].h[:], 0.0), [], [S[0][:]])
    Ut = [kb.sb([64, 64], F32, f"U{s}") for s in range(2)]
    psb = kb.ps([64, 8, 512], F32, "psb2")
    pb = [0]

    def ibank():
        b = pb[0] % 5
        pb[0] += 1
        return b

    def P3(b):
        return V(psb.h[:, b, :].rearrange("p (g c) -> p g c", c=64), ("psb2", b))

    def P2(b):
        return V(psb.h[:, b, :], ("psb2", b))

    def F2(tl):
        return V(tl.h[:].rearrange("p g c -> p (g c)"), tl.name)

    def Pc(b, c):
        return V(psb.h[:, b, c * 64:(c + 1) * 64], ("psb2", b))

    evq = [0]

    def ev_eng():
        evq[0] += 1
        return "dve" if evq[0] % 2 == 0 else "pool"

    def load(g):
        s = g % 3
        for n in fm_names:
            kb.dma("sp", f"ld_f{s}", fm[s][n][:], fm_in[n][:, g * GRP * CH:(g + 1) * GRP * CH])
        for n in tm_names:
            kb.dma("sp", f"ld_t{s}", tm[s][n][:], tm_in[n][:, g * GRP:(g + 1) * GRP, :])

    def mm8(bnk, lhs_fn, rhs_fn):
        for c in range(GRP):
            kb.mm(Pc(bnk, c), lhs_fn(c), rhs_fn(c))

    def indep_units(g):
        s = g % 2
        f, t, w = fm[g % 3], tm[g % 3], wk[s]

        def fmc(n):
            return lambda c: V(f[n].h[:, c * CH:(c + 1) * CH], f[n].name)

        def tc(tl):
            return lambda c: V(tl.h[:, c, :], tl.name)

        units = []

        def u_masked(out, lhs, rhs, mask_i):
            def fn():
                b = ibank()
                mm8(b, lhs, rhs)
                kb.tt("dve", out[:], P3(b), cb(mask_i), ALU.mult)
            return fn

        def u_M0():
            b = ibank()
            mm8(b, fmc("bT"), fmc("aT"))
            kb.tt("dve", w["M0"][:], P3(b), cb(1), ALU.mult)
            kb.tt("pool", w["P0"][:], w["M0"][:], cb(0), ALU.add)

        def u_N0():
            b = ibank()
            mm8(b, fmc("aT"), fmc("bT"))
            kb.tt("dve", w["N0"][:], P3(b), cb(3), ALU.mult)

        units += [u_M0, u_N0]
        for lvl in range(5):
            Mi, Mo = w[f"M{lvl % 2}"], w[f"M{(lvl + 1) % 2}"]
            Ni, No = w[f"N{lvl % 2}"], w[f"N{(lvl + 1) % 2}"]
            Pi, Po = w[f"P{lvl % 2}"], w[f"P{(lvl + 1) % 2}"]

            def u_N2(Mi=Mi, Ni=Ni, No=No):
                b = ibank()
                mm8(b, tc(Mi), tc(Ni))
                kb.copy("act", F2(No), P2(b))
                kb.tt("pool", w["IN"][:], No[:], cb(0), ALU.add)

            def u_M2(Mi=Mi, Ni=Ni, Mo=Mo):
                b = ibank()
                mm8(b, tc(Ni), tc(Mi))
                kb.copy("act", F2(Mo), P2(b))

            def u_P(Pi=Pi, Po=Po):
                b = ibank()
                mm8(b, tc(w["IN"]), tc(Pi))
                kb.copy("act", F2(Po), P2(b))

            units.append(u_N2)
            if lvl < 4:
                units.append(u_M2)
            units.append(u_P)
        TT = w["P1"]
        units.append(u_masked(w["Aak"], fmc("kT"), fmc("aT"), 1))
        units.append(u_masked(w["Arb"], fmc("bT"), fmc("rT"), 2))
        units.append(u_masked(w["Ark"], fmc("kT"), fmc("rT"), 2))

        def u_Abar():
            b = ibank()
            mm8(b, tc(t["A"]), tc(TT))
            kb.copy("act", F2(w["Abar"]), P2(b))

        def u_W():
            b = ibank()
            mm8(b, tc(w["Aak"]), tc(t["Vv"]))
            kb.copy("act", F2(w["W"]), P2(b))

        units += [u_Abar, u_W]
        return units

    dstate = dict(cur=0, n=0)

    def dep_units(g):
        s = g % 2
        f, t, w = fm[g % 3], tm[g % 3], wk[s]
        TT = w["P1"]
        units = []
        for c in range(GRP):
            def d1(c=c):
                S0 = S[dstate["cur"]]
                U = Ut[dstate["n"] % 2]
                ub = 5
                ucol = dstate["n"] % 8
                kb.mm(Pc(ub, ucol), V(TT.h[:, c, :], TT.name), V(w["W"].h[:, c, :], w["W"].name), start=True, stop=False)
                kb.mm(Pc(ub, ucol), V(w["Abar"].h[:, c, :], w["Abar"].name), S0[:], start=False, stop=True)
                kb.copy("act", U[:], Pc(ub, ucol))

            def d2(c=c):
                S0 = S[dstate["cur"]]
                S1 = S[1 - dstate["cur"]]
                U = Ut[dstate["n"] % 2]
                col = dstate["n"] % 8
                rT = V(f["rT"].h[:, c * CH:(c + 1) * CH], f["rT"].name)
                Vc = V(t["Vv"].h[:, c, :], t["Vv"].name)
                kb.mm(Pc(6, col), rT, S0[:], start=True, stop=False)
                kb.mm(Pc(6, col), V(w["Ark"].h[:, c, :], w["Ark"].name), Vc, start=False, stop=False)
                kb.mm(Pc(6, col), V(w["Arb"].h[:, c, :], w["Arb"].name), U[:], start=False, stop=True)
                kb.mm(Pc(7, col), V(t["Kh"].h[:, c, :], t["Kh"].name), Vc, start=True, stop=False)
                kb.mm(Pc(7, col), V(t["Bh"].h[:, c, :], t["Bh"].name), U[:], start=False, stop=True)
                ch = g * GRP + c
                kb.stt(S1[:], S0[:], gc[:, ch:ch + 1], Pc(7, col), ALU.mult, ALU.add)
                kb.copy("pool" if False else "act", V(ytile[s].h[:, c, :], ytile[s].name), Pc(6, col))
                dstate["cur"] = 1 - dstate["cur"]
                dstate["n"] += 1
                if c == GRP - 1:
                    kb.dma("sp", f"st_y{s}", y_out[:, g * GRP:(g + 1) * GRP, :], ytile[s][:], is_output=True)

            units += [d1, d2]
        return units

    load(0)
    for g in range(ngrp + 1):
        iu = []
        du = []
        if g < ngrp:
            if g + 1 < ngrp:
                pass
            iu = indep_units(g)
        if g >= 1:
            du = dep_units(g - 1)
        if g + 1 < ngrp:
            load(g + 1) if g >= 0 else None
        if dbg.get("nodep"):
            du = []
        if "ndu" in dbg:
            du = du[:dbg["ndu"]]
        if "niu" in dbg:
            iu = iu[:dbg["niu"]]
        n = max(len(iu), len(du))
        for i in range(n):
            if i < len(du):
                du[i]()
            if i < len(iu):
                iu[i]()
    if dbg:
        kb.dma("sp", "st_y0", y_out[:, 0:GRP, :], wk[0][dbg.get("outw", "M0")][:], is_output=True)
    kb.finish()
    return kb


def prep_l2(l1res, nchk=NCHK):
    ntok = nchk * CH

    def full(name):
        return np.concatenate([l1res[c][name].transpose(1, 0, 2).reshape(512, -1) for c in range(NCORES)], axis=1)

    F = {n: full(n) for n in ("o_at", "o_bt", "o_kt", "o_rt", "o_bh", "o_kh", "o_v", "o_gc")}
    cst = np.zeros((64, 4, 64), np.float32)
    cst[:, 0, :] = np.eye(64)
    cst[:, 1, :] = np.triu(np.ones((64, 64)), 1)
    cst[:, 2, :] = np.triu(np.ones((64, 64)), 0)
    cst[:, 3, :] = np.tril(np.ones((64, 64)), -1)
    maps = []
    for h in range(NCORES):
        sl = slice(64 * h, 64 * h + 64)

        def tmaj(a):
            return np.ascontiguousarray(a[:, :ntok].T.reshape(nchk, CH, 64).transpose(1, 0, 2))

        m = dict(aT=np.ascontiguousarray(F["o_at"][sl, :ntok]), bT=np.ascontiguousarray(F["o_bt"][sl, :ntok]),
                 kT=np.ascontiguousarray(F["o_kt"][sl, :ntok]), rT=np.ascontiguousarray(F["o_rt"][sl, :ntok]),
                 A=tmaj(F["o_at"][sl]), Vv=tmaj(F["o_v"][sl]), Bh=tmaj(F["o_bh"][sl]), Kh=tmaj(F["o_kh"][sl]),
                 gc=np.ascontiguousarray(F["o_gc"][sl, :nchk]), cst2=cst)
        maps.append(m)
    return maps


TB = 512
NTB = TPC // TB
PV = dict(g_ffn=0, g_ple=8, g_next=16, ln_w=24, ln_b=28)
PV_N = 32


def _wspecs(layer):
    sp = [("w_up", 32, 8), ("w_dn", 8, 32), ("w_pg", 8, 8), ("w_pp", 8, 2)]
    if layer == 0:
        sp = [("w_mix", 8, 8)] + sp
    else:
        sp = [("w_g1", 8, 8), ("w_g2", 8, 8)] + sp
    return sp


def build_post(layer, dbg=None):
    kb = KB()
    dbg = dbg or {}
    skip = dbg.get('skip', ())
    specs = _wspecs(layer)
    w_in = {n: kb.dram(n, [M, 128, KT * 128], F32, "ExternalInput") for n, M, KT in specs}
    w_bf = {n: kb.dram(n + "_bf", [M, 128, KT * 128], BF16, "Internal") for n, M, KT in specs}
    wdim = {n: (M, KT) for n, M, KT in specs}
    vec_in = kb.dram("pvec", [128, PV_N], F32, "ExternalInput")
    cst_in = kb.dram("pcst", [128, 128], F32, "ExternalInput")
    hin = kb.dram("hin", [128, 8, TPC], F32, "ExternalInput")
    pin = kb.dram("pin", [128, 2, TPC], F32, "ExternalInput")
    if layer == 0:
        a_in = {n: kb.dram(n, [128, 4, TPC], F32, "ExternalInput") for n in ("yrec", "gg", "bon", "ybb")}
        h_out = kb.dram("h_out", [128, 8, TPC], F32, "ExternalOutput")
    else:
        yg_in = kb.dram("yg", [128, 8, TPC], F32, "ExternalInput")
    n_out = kb.dram("n_out", [128, 8, TPC], F32, "ExternalOutput")

    vec = kb.sb([128, PV_N], F32, "pvec_s")
    kb.dma("sp", "ld_c", vec[:], vec_in)
    cst = kb.sb([128, 128], F32, "pcst_s")
    kb.dma("sp", "ld_c", cst[:], cst_in)
    ones_bf = kb.sb([128, 128], BF16, "ones_bf")
    kb.op("pool", lambda e: e.memset(ones_bf.h[:], 1.0), [], [ones_bf[:]])
    epsc = kb.sb([128, 2], F32, "epsc")
    kb.op("pool", lambda e: e.memset(epsc.h[:, 0:1], RMS_EPS), [], [epsc[:]])
    kb.op("pool", lambda e: e.memset(epsc.h[:, 1:2], GN_EPS), [], [epsc[:]])

    def vc(name, i=0):
        c = PV[name] + i
        return vec[:, c:c + 1]

    SW = 2048
    st32 = [kb.sb([128, SW], F32, f"st32_{i}") for i in range(2)]
    st16 = [kb.sb([128, SW], BF16, f"st16_{i}") for i in range(2)]
    i = 0
    for n, M, KT in specs:
        W = KT * 128
        if W <= SW:
            per = SW // W
            jobs = [(m0, min(per, M - m0), 0, W) for m0 in range(0, M, per)]
        else:
            jobs = [(m0, 1, w0, SW) for m0 in range(M) for w0 in range(0, W, SW)]
        for (m0, k, w0, ww) in ([] if 'prepass' in skip else jobs):
            s = i % 2
            src = w_in[n][m0:m0 + k, :, w0:w0 + ww].rearrange("m p w -> p m w")
            dst = w_bf[n][m0:m0 + k, :, w0:w0 + ww].rearrange("m p w -> p m w")
            a32_ = V(st32[s].h[:, 0:k * ww].rearrange("p (m w) -> p m w", w=ww), st32[s].name)
            a16_ = V(st16[s].h[:, 0:k * ww].rearrange("p (m w) -> p m w", w=ww), st16[s].name)
            kb.dma("sp", f"wc_ld{s}", a32_, src)
            eng = ("act", "dve", "pool")[i % 3]
            kb.copy(eng, V(st16[s].h[:, 0:k * ww], st16[s].name), V(st32[s].h[:, 0:k * ww], st32[s].name))
            kb.dma("sp", f"wc_st{s}", dst, a16_)
            i += 1
    for s in range(2):
        if f"wc_st{s}" in kb.dsem:
            kb.E["sp"].wait_ge(kb.dsem[f"wc_st{s}"], kb.dcnt[f"wc_st{s}"])

    wring = {}
    wctr = {}
    for n, M, KT in specs:
        depth = 3 if KT <= 8 else 2
        wring[n] = [kb.sb([128, KT, 128], BF16, f"wr_{n}{d}") for d in range(depth)]
        wctr[n] = 0
    wq = [0]

    def wpiece(n, m):
        ring = wring[n]
        d = wctr[n] % len(ring)
        wctr[n] += 1
        t = ring[d]
        kb.dma("sp", f"wl_{n}{d}", V(t.h[:].rearrange("p k q -> p (k q)"), t.name), w_bf[n][m])
        return t

    psb = kb.ps([128, 8, 512], F32, "psb3")
    pbk = [0]

    def bank():
        b = pbk[0] % 8
        pbk[0] += 1
        return b

    def P(b):
        return V(psb.h[:, b, 0:TB], ("psb3", b))

    h = kb.sb([128, 8, TB], F32, "h")
    hn = kb.sb([128, 8, TB], BF16, "hn")
    sq = kb.sb([128, 8, TB], BF16, "sq")
    rstd = kb.sb([128, TB], F32, "rstd")
    act8 = kb.sb([128, 8, TB], BF16, "act8")
    a32 = kb.sb([128, 32, TB], BF16, "a32")
    pb32 = kb.sb([128, 2, TB], F32, "pb32")
    pb16 = kb.sb([128, 2, TB], BF16, "pb16")
    tmpf = [kb.sb([128, TB], F32, f"tmpf{i}") for i in range(4)]
    stg = [kb.sb([128, 4, TB], F32, f"stg{i}") for i in range(4)]
    tq = [0]

    def tmp():
        tq[0] += 1
        return tmpf[tq[0] % 4]

    def rmsnorm(gname, out_tile, split=None):
        for kt in range(8):
            kb.act(sq[:, kt, :], h[:, kt, :], AF.Square)
        bn = bank()
        for kt in range(8):
            kb.mm(P(bn), ones_bf[:], sq[:, kt, :], start=(kt == 0), stop=(kt == 7))
        kb.act(rstd[:], P(bn), AF.Sqrt, bias=epsc[:, 0:1], scale=1.0 / D)
        kb.op("dve", lambda e: e.reciprocal(out=rstd.h[:], in_=rstd.h[:]), [rstd[:]], [rstd[:]])
        for kt in range(8):
            o_ = out_tile[:, kt, :] if split is None else split[kt // 4][:, kt % 4, :]
            kb.stt(o_, h[:, kt, :], vc(gname, kt), rstd[:], ALU.mult, ALU.mult)

    def dense(n, m, rhs_tile, bn):
        M, KT = wdim[n]
        wt = wpiece(n, m)
        for kt in range(KT):
            kb.mm(P(bn), V(wt.h[:, kt, :], wt.name), rhs_tile[:, kt, :], start=(kt == 0), stop=(kt == KT - 1))

    for b in range(dbg.get('nblk', NTB)):
        ts_ = slice(b * TB, (b + 1) * TB)
        kb.dma("sp", "ld_h", h[:], hin[:, :, ts_])
        kb.dma("sp", "ld_p", pb32[:], pin[:, :, ts_])
        if 'mixer' in skip:
            pass
        elif layer == 0:
            for i, n in enumerate(("yrec", "gg", "bon", "ybb")):
                kb.dma("sp", f"ld_a{i}", stg[i][:], a_in[n][:, :, ts_])
            yrec, gg, bon, ybb = stg
            for p in range(4):
                y = V(yrec.h[:, p, :], yrec.name)
                bn = bank()
                kb.mm(P(bn), cst[:], y)
                yc = tmp()
                kb.stt(yc[:], P(bn), -1.0 / 64, y, ALU.mult, ALU.add)
                s2 = tmp()
                kb.act(s2[:], yc[:], AF.Square)
                bn = bank()
                kb.mm(P(bn), cst[:], s2[:])
                sd = tmp()
                kb.act(sd[:], P(bn), AF.Sqrt, bias=epsc[:, 1:2], scale=1.0 / 64)
                kb.op("dve", lambda e, sd=sd: e.reciprocal(out=sd.h[:], in_=sd.h[:]), [sd[:]], [sd[:]])
                kb.tt("dve", yc[:], yc[:], sd[:], ALU.mult)
                kb.ts("dve", yc[:], yc[:], vc("ln_w", p), ALU.mult, vc("ln_b", p), ALU.add)
                kb.tt("pool", yc[:], yc[:], V(bon.h[:, p, :], bon.name), ALU.add)
                kb.tt("pool", act8[:, p, :], yc[:], V(gg.h[:, p, :], gg.name), ALU.mult)
                kb.copy("act", act8[:, 4 + p, :], V(ybb.h[:, p, :], ybb.name))
            for m in range(8):
                bn = bank()
                dense("w_mix", m, act8, bn)
                kb.tt("dve", h[:, m, :], h[:, m, :], P(bn), ALU.add)
        else:
            for half in range(2):
                kb.dma("sp", f"ld_a{half}", stg[half][:], yg_in[:, 4 * half:4 * half + 4, ts_])
                kb.copy("act" if half == 0 else "pool", act8[:, 4 * half:4 * half + 4, :], stg[half][:])
            for m in range(8):
                b1, b2 = bank(), bank()
                dense("w_g1", m, act8, b1)
                dense("w_g2", m, act8, b2)
                sg = tmp()
                kb.act(sg[:], P(b2), AF.Sigmoid)
                kb.tt("dve", sg[:], sg[:], P(b1), ALU.mult)
                kb.tt("pool", h[:, m, :], h[:, m, :], sg[:], ALU.add)
        if layer == 0 and False:
            pass
        rmsnorm("g_ffn", hn)
        for m in range(0 if 'ffn' in skip else 32):
            bn = bank()
            dense("w_up", m, hn, bn)
            r = tmp()
            kb.act(r[:], P(bn), AF.Relu)
            kb.tt("pool", a32[:, m, :], r[:], r[:], ALU.mult)
        for m in range(0 if 'ffn' in skip else 8):
            bn = bank()
            dense("w_dn", m, a32, bn)
            kb.tt("dve", h[:, m, :], h[:, m, :], P(bn), ALU.add)
        rmsnorm("g_ple", hn)
        kb.copy("act", pb16[:], pb32[:])
        for m in range(0 if 'ple' in skip else 8):
            b1, b2 = bank(), bank()
            dense("w_pg", m, hn, b1)
            dense("w_pp", m, pb16, b2)
            sg = tmp()
            kb.act(sg[:], P(b1), AF.Sigmoid)
            kb.tt("dve", sg[:], sg[:], P(b2), ALU.mult)
            kb.tt("pool", h[:, m, :], h[:, m, :], sg[:], ALU.add)
        if layer == 0:
            kb.dma("sp", "st_h", h_out[:, :, ts_], h[:], is_output=True)
        rmsnorm("g_next", None, split=(stg[2], stg[3]))
        kb.dma("sp", "st_n", n_out[:, 0:4, ts_], stg[2][:], is_output=True)
        kb.dma("sp", "st_n", n_out[:, 4:8, ts_], stg[3][:], is_output=True)
    kb.finish()
    return kb


def _wpieces(W):
    K, N = W.shape
    KT, M = K // 128, N // 128
    return np.ascontiguousarray(W.reshape(KT, 128, M, 128).transpose(2, 1, 0, 3).reshape(M, 128, KT * 128))


def _fm(a):
    T_, F_ = a.shape
    return np.ascontiguousarray(a.T.reshape(F_ // 128, 128, T_).transpose(1, 0, 2))


def _unfm(a):
    return np.ascontiguousarray(a.transpose(2, 1, 0).reshape(a.shape[2], -1))


def prep_post(layer, inp, hin_tok, extra):
    L = f"l{layer}_"
    ws = dict(w_up=_wpieces(inp[L + "ffn_up"]), w_dn=_wpieces(inp[L + "ffn_down"]),
              w_pg=_wpieces(inp[L + "ple_gate"]), w_pp=_wpieces(inp[L + "ple_proj"]))
    if layer == 0:
        ws["w_mix"] = _wpieces(inp["l0_w_out"])
    else:
        ws["w_g1"] = _wpieces(inp["l1_glu_w1"])
        ws["w_g2"] = _wpieces(inp["l1_glu_w2"])
    vec = np.zeros((128, PV_N), np.float32)

    def col(v, n):
        return v.reshape(n, 128).T

    vec[:, 0:8] = col(inp[L + "norm_ffn"], 8)
    vec[:, 8:16] = col(inp[L + "norm_ple"], 8)
    vec[:, 16:24] = col(inp["l1_norm_mix"] if layer == 0 else inp["norm_final"], 8)
    if layer == 0:
        vec[:, 24:28] = col(inp["l0_ln_w"], 4)
        vec[:, 28:32] = col(inp["l0_ln_b"], 4)
    cst = np.zeros((128, 128), np.float32)
    cst[0:64, 0:64] = 1.0
    cst[64:128, 64:128] = 1.0
    p_tok = inp["p"][layer, 0]
    maps = []
    for c in range(NCORES):
        sl = slice(c * TPC, (c + 1) * TPC)
        m = dict(ws)
        m.update(pvec=vec, pcst=cst, hin=_fm(hin_tok[sl]), pin=_fm(p_tok[sl]))
        for k, v in extra.items():
            m[k] = _fm(v[sl])
        maps.append(m)
    return maps


SCB = 32


def build_s5(nb=SEQ // 8):
    kb = KB()
    nsc = nb // SCB
    npc = nb // 512
    vin = kb.dram("vin", [8, 128, nb], F32, "ExternalInput")
    lam_in = kb.dram("lam", [128, 4, 2], F32, "ExternalInput")
    dl_in = kb.dram("dl", [128, 4], F32, "ExternalInput")
    bri_in = kb.dram("bri", [128, 4, 2, 16], F32, "ExternalInput")
    ct_in = kb.dram("ct", [128, 4, 2, 16], F32, "ExternalInput")
    dsk_in = kb.dram("dsk", [128, 8], F32, "ExternalInput")
    msk_in = kb.dram("msk", [128, 2, 128], F32, "ExternalInput")
    yout = kb.dram("yout", [8, 128, nb], F32, "ExternalOutput")

    def ld(name, shape, src):
        t = kb.sb(shape, F32, name)
        kb.dma("sp", "ld_c", t[:], src)
        return t

    lam = ld("lam_s", [128, 4, 2], lam_in)
    dl = ld("dl_s", [128, 4], dl_in)
    bri = ld("bri_s", [128, 4, 2, 16], bri_in)
    ct = ld("ct_s", [128, 4, 2, 16], ct_in)
    dsk = ld("dsk_s", [128, 8], dsk_in)
    msk = ld("msk_s", [128, 2, 128], msk_in)
    hpi = kb.sb([128, 1], F32, "hpi")
    kb.op("pool", lambda e: e.memset(hpi.h[:], math.pi / 2), [], [hpi[:]])

    nsm = [0]

    def sm(shape, name=None):
        nsm[0] += 1
        return kb.sb(shape, F32, name or f"sm{nsm[0]}")

    E_ = "dve"

    def tt(o, a, b, op, e=None):
        kb.tt(e or E_, o, a, b, op)

    scr = [sm([128, 4, 64], f"scr{i}") for i in range(4)]

    def sview(i, shp):
        n = 1
        for d in shp[1:]:
            n *= d
        flat = scr[i].h[:].rearrange("p a b -> p (a b)")[:, 0:n]
        if len(shp) == 3:
            flat = flat.rearrange("p (a b) -> p a b", b=shp[2])
        return V(flat, scr[i].name)

    def cmul(o_r, o_i, ar, ai, br, bi, shp):
        t1, t2, t3, t4 = (sview(i, shp) for i in range(4))
        tt(t1, ar, br, ALU.mult)
        tt(t2, ai, bi, ALU.mult, "pool")
        tt(t3, ar, bi, ALU.mult)
        tt(t4, ai, br, ALU.mult, "pool")
        tt(o_r, t1, t2, ALU.subtract)
        tt(o_i, t3, t4, ALU.add, "pool")

    S4 = [128, 4]
    lre = sm(S4, "lre")
    kb.ts("dve", lre[:], V(lam.h[:, :, 0], lam.name), -1e-4, ALU.min)
    lim = V(lam.h[:, :, 1], lam.name)
    dlt = sm(S4, "dlt")
    kb.act(dlt[:], dl[:], AF.Exp)
    xr = sm(S4, "xr")
    th = sm(S4, "th")
    tt(xr[:], lre[:], dlt[:], ALU.mult)
    tt(th[:], lim, dlt[:], ALU.mult)
    mag = sm(S4, "mag")
    kb.act(mag[:], xr[:], AF.Exp)
    cs = sm([128, 2, 4], "cs")
    c_, s_ = V(cs.h[:, 0, :], cs.name), V(cs.h[:, 1, :], cs.name)
    kb.act(s_, th[:], AF.Sin, scale=1.0 / 16)
    kb.act(c_, th[:], AF.Sin, scale=1.0 / 16, bias=hpi[:, 0:1])
    c2 = sm([128, 2, 4], "cs2")
    for it in range(4):
        src, dst = (cs, c2) if it % 2 == 0 else (c2, cs)
        cr, si = V(src.h[:, 0, :], src.name), V(src.h[:, 1, :], src.name)
        cmul(V(dst.h[:, 0, :], dst.name), V(dst.h[:, 1, :], dst.name), cr, si, cr, si, S4)
    PWr, PWi = sm([128, 4, 9], "PWr"), sm([128, 4, 9], "PWi")
    IPr, IPi = sm([128, 4, 9], "IPr"), sm([128, 4, 9], "IPi")

    def k_(t, k):
        return V(t.h[:, :, k], t.name)

    for t_, v in ((PWr, 1.0), (PWi, 0.0), (IPr, 1.0), (IPi, 0.0)):
        kb.op("pool", lambda e, t_=t_, v=v: e.memset(t_.h[:, :, 0:1], v), [], [t_[:]])
    tt(k_(PWr, 1), mag[:], c_, ALU.mult)
    tt(k_(PWi, 1), mag[:], s_, ALU.mult)
    for k in range(2, 9):
        cmul(k_(PWr, k), k_(PWi, k), k_(PWr, k - 1), k_(PWi, k - 1), k_(PWr, 1), k_(PWi, 1), S4)
    m2 = sm(S4, "m2")
    tt(m2[:], mag[:], mag[:], ALU.mult)
    kb.op("dve", lambda e: e.reciprocal(out=m2.h[:], in_=m2.h[:]), [m2[:]], [m2[:]])
    tt(k_(IPr, 1), k_(PWr, 1), m2[:], ALU.mult)
    nli = sm(S4, "nli")
    kb.ts("dve", nli[:], k_(PWi, 1), -1.0, ALU.mult)
    tt(k_(IPi, 1), nli[:], m2[:], ALU.mult)
    for k in range(2, 9):
        cmul(k_(IPr, k), k_(IPi, k), k_(IPr, k - 1), k_(IPi, k - 1), k_(IPr, 1), k_(IPi, 1), S4)
    nr = sm(S4, "nr")
    kb.ts("dve", nr[:], k_(PWr, 1), -1.0, ALU.add)
    den = sm(S4, "den")
    d2 = sm(S4, "d2")
    tt(den[:], lre[:], lre[:], ALU.mult)
    tt(d2[:], lim, lim, ALU.mult)
    tt(den[:], den[:], d2[:], ALU.add)
    kb.op("dve", lambda e: e.reciprocal(out=den.h[:], in_=den.h[:]), [den[:]], [den[:]])
    nlim = sm(S4, "nlim")
    kb.ts("dve", nlim[:], lim, -1.0, ALU.mult)
    rr, ri = sm(S4, "rr"), sm(S4, "ri")
    cmul(rr[:], ri[:], nr[:], k_(PWi, 1), lre[:], nlim[:], S4)
    tt(rr[:], rr[:], den[:], ALU.mult)
    tt(ri[:], ri[:], den[:], ALU.mult)
    S416 = [128, 4, 16]
    Bbr, Bbi = sm(S416, "Bbr"), sm(S416, "Bbi")

    def bc16(t):
        return V(t.h[:].unsqueeze(2).to_broadcast(S416), t.name)

    cmul(Bbr[:], Bbi[:], bc16(rr), bc16(ri), V(bri.h[:, :, 0, :], bri.name), V(bri.h[:, :, 1, :], bri.name), S416)
    CTr, CTi = V(ct.h[:, :, 0, :], ct.name), V(ct.h[:, :, 1, :], ct.name)
    NJ = SCB
    Zr, Zi = sm([128, 4, NJ], "Zr"), sm([128, 4, NJ], "Zi")
    Rj = sm([128, 4, NJ], "Rj")
    kb.copy("pool", V(Zr.h[:, :, 0], Zr.name), k_(PWr, 8))
    kb.copy("pool", V(Zi.h[:, :, 0], Zi.name), k_(PWi, 8))
    rho8 = sm(S4, "rho8")
    tt(rho8[:], mag[:], mag[:], ALU.mult)
    tt(rho8[:], rho8[:], rho8[:], ALU.mult)
    tt(rho8[:], rho8[:], rho8[:], ALU.mult)
    kb.copy("pool", V(Rj.h[:, :, 0], Rj.name), rho8[:])
    ln = 1
    while ln < NJ:
        shp = [128, 4, ln]

        def sl(t, a, b_):
            return V(t.h[:, :, a:b_], t.name)

        def bcl(t, k):
            return V(t.h[:, :, k:k + 1].to_broadcast(shp), t.name)

        cmul(sl(Zr, ln, 2 * ln), sl(Zi, ln, 2 * ln), sl(Zr, 0, ln), sl(Zi, 0, ln), bcl(Zr, ln - 1), bcl(Zi, ln - 1), shp)
        tt(sl(Rj, ln, 2 * ln), sl(Rj, 0, ln), bcl(Rj, ln - 1), ALU.mult)
        ln *= 2
    Ur, Ui = sm([128, 4, NJ], "Ur"), sm([128, 4, NJ], "Ui")
    Rinv = sm([128, 4, NJ], "Rinv")
    kb.op("dve", lambda e: e.reciprocal(out=Rinv.h[:], in_=Rj.h[:]), [Rj[:]], [Rinv[:]])
    tt(Ur[:], Zr[:], Rinv[:], ALU.mult)
    tt(Ui[:], Zi[:], Rinv[:], ALU.mult)
    NK = max(nsc, 2)
    Wr, Wi = sm([128, 4, NK], "Wr"), sm([128, 4, NK], "Wi")
    kb.copy("pool", V(Wr.h[:, :, 0], Wr.name), V(Ur.h[:, :, NJ - 1], Ur.name))
    kb.copy("pool", V(Wi.h[:, :, 0], Wi.name), V(Ui.h[:, :, NJ - 1], Ui.name))
    ln = 1
    while ln < nsc:
        l2 = min(ln, nsc - ln)
        shp = [128, 4, l2]
        cmul(V(Wr.h[:, :, ln:ln + l2], Wr.name), V(Wi.h[:, :, ln:ln + l2], Wi.name),
             V(Wr.h[:, :, 0:l2], Wr.name), V(Wi.h[:, :, 0:l2], Wi.name),
             V(Wr.h[:, :, ln - 1:ln].to_broadcast(shp), Wr.name), V(Wi.h[:, :, ln - 1:ln].to_broadcast(shp), Wi.name), shp)
        ln *= 2
    def neg(t, shp, name):
        o = sm(shp, name)
        kb.ts("pool", o[:], t[:], -1.0, ALU.mult)
        return o

    nPWr, nPWi = neg(PWr, [128, 4, 9], "nPWr"), neg(PWi, [128, 4, 9], "nPWi")
    nIPi = neg(IPi, [128, 4, 9], "nIPi")
    nUi = neg(Ui, [128, 4, NJ], "nUi")
    nWi = neg(Wi, [128, 4, NK], "nWi")
    nZi = neg(Zi, [128, 4, NJ], "nZi")

    ChRe = [kb.sb([128, 8, 16], BF16, f"ChRe{p}") for p in range(4)]
    ChIm = [kb.sb([128, 8, 16], BF16, f"ChIm{p}") for p in range(4)]
    BhRe = [kb.sb([128, 128], BF16, f"BhRe{p}") for p in range(4)]
    BhIm = [kb.sb([128, 128], BF16, f"BhIm{p}") for p in range(4)]
    Tm = [kb.sb([128, 128], BF16, f"Tm{g}") for g in range(8)]
    psb = kb.ps([128, 8, 512], F32, "psb4")
    pbk = [0]

    def bank():
        b = pbk[0] % 8
        pbk[0] += 1
        return b

    def Pb(b, n=512, lo=0):
        return V(psb.h[:, b, lo:lo + n], ("psb4", b))

    f32t = [kb.sb([128, 8, 16], F32, f"f32t{i}") for i in range(6)]
    for p in range(4):
        def col(t, k):
            return V(t.h[:, p, k:k + 1], t.name)

        def pv(x):
            return V(x.ap[:, p, :], x.res)

        cre, cim = pv(CTr), pv(CTi)
        bre, bim = V(Bbr.h[:, p, :], Bbr.name), V(Bbi.h[:, p, :], Bbi.name)
        fChRe, fChIm, fBmRe, fBmIm, fBtRe, fBtIm = f32t
        for t in range(8):
            def o(tl):
                return V(tl.h[:, t, :], tl.name)
            kb.ts("dve", o(fChRe), cre, col(PWr, t + 1), ALU.mult)
            kb.stt(o(fChRe), cim, col(nPWi, t + 1), o(fChRe), ALU.mult, ALU.add)
            kb.ts("pool", o(fChIm), cre, col(nPWi, t + 1), ALU.mult)
            kb.stt(o(fChIm), cim, col(nPWr, t + 1), o(fChIm), ALU.mult, ALU.add)
            kb.ts("dve", o(fBmRe), bre, col(IPr, t + 1), ALU.mult)
            kb.stt(o(fBmRe), bim, col(nIPi, t + 1), o(fBmRe), ALU.mult, ALU.add)
            kb.ts("pool", o(fBmIm), bre, col(IPi, t + 1), ALU.mult)
            kb.stt(o(fBmIm), bim, col(IPr, t + 1), o(fBmIm), ALU.mult, ALU.add)
            kb.ts("dve", o(fBtRe), bre, col(PWr, 7 - t), ALU.mult)
            kb.stt(o(fBtRe), bim, col(nPWi, 7 - t), o(fBtRe), ALU.mult, ALU.add)
            kb.ts("pool", o(fBtIm), bre, col(PWi, 7 - t), ALU.mult)
            kb.stt(o(fBtIm), bim, col(PWr, 7 - t), o(fBtIm), ALU.mult, ALU.add)
        kb.copy("act", ChRe[p][:], fChRe[:])
        kb.copy("act", ChIm[p][:], fChIm[:])
        for (src, dst) in ((fBtRe, BhRe[p]), (fBtIm, BhIm[p])):
            bn = bank()
            kb.op("pe", lambda e, src=src, bn=bn: e.transpose(psb.h[:, bn, 0:128], src.h[:].rearrange("p a b -> p (a b)"),
                                                              msk.h[:, 1, :]), [src[:], msk[:]], [Pb(bn)])
            kb.copy("act", dst[:], Pb(bn, 128))
        for gg in range(2):
            rows = slice(64 * gg, 64 * gg + 64)
            bn = bank()

            def rr_(tl):
                return V(tl.h[rows].rearrange("p a b -> p (a b)"), tl.name)

            kb.mm(Pb(bn, 128), rr_(fBmRe), rr_(fChRe), start=True, stop=False)
            kb.mm(Pb(bn, 128), rr_(fBmIm), rr_(fChIm), start=False, stop=True)
            kb.tt("dve", Tm[2 * p + gg][:], Pb(bn, 128), V(msk.h[:, 0, :], msk.name), ALU.mult)

    rmask = kb.sb([128, nb], F32, "rmask")
    kb.op("pool", lambda e: e.memset(rmask.h[:], 1.0), [], [rmask[:]])
    kb.op("pool", lambda e: e.memset(rmask.h[:].rearrange("p (k j) -> p k j", j=SCB)[:, :, 0:1], 0.0), [], [rmask[:]])
    d0 = kb.sb([128, nb], F32, "d0")
    V32 = [kb.sb([128, nb], F32, f"V32_{i}") for i in range(2)]
    Vbf = [kb.sb([128, nb], BF16, f"Vbf_{i}") for i in range(2)]
    Xr, Xi = kb.sb([128, nb], F32, "Xr"), kb.sb([128, nb], F32, "Xi")
    Mr, Mi = kb.sb([128, nb], F32, "Mr"), kb.sb([128, nb], F32, "Mi")
    t1, t2 = kb.sb([128, nb], F32, "t1"), kb.sb([128, nb], F32, "t2")
    Hr, Hi = kb.sb([128, nb], BF16, "Hr"), kb.sb([128, nb], BF16, "Hi")
    Er, Ei = sm([128, NK], "Er"), sm([128, NK], "Ei")
    Fr, Fi = sm([128, NK + 1], "Fr"), sm([128, NK + 1], "Fi")
    e1, e2 = sm([128, NK], "e1"), sm([128, NK], "e2")
    ypre = [kb.sb([128, 512], F32, f"ypre{i}") for i in range(2)]
    gt = [kb.sb([128, 512], F32, f"gt{i}") for i in range(2)]

    def v3(t):
        return V(t.h[:].rearrange("p (k j) -> p k j", j=SCB), t.name)

    def tabj(t, p):
        return V(t.h[:, p:p + 1, :].to_broadcast([128, nsc, SCB]), t.name)

    for p in range(4):
        for gg in range(2):
            g = 2 * p + gg
            kb.dma("sp", f"ld_v{gg}", V32[gg][:], vin[g])
            kb.copy("act" if gg == 0 else "pool", Vbf[gg][:], V32[gg][:])
        for pc in range(npc):
            cs_ = slice(pc * 512, (pc + 1) * 512)
            for (Bh, X, eng) in ((BhRe[p], Xr, "act"), (BhIm[p], Xi, "act")):
                bn = bank()
                for gg in range(2):
                    kb.mm(V(psb.h[64 * gg:64 * gg + 64, bn, :], ("psb4", bn)), Bh[:, 64 * gg:64 * gg + 64],
                          Vbf[gg][:, cs_])
                kb.copy(eng, X[:, cs_], Pb(bn))
        tt(v3(Mr), v3(Xr), tabj(Ur, p), ALU.mult)
        tt(v3(t1), v3(Xi), tabj(Ui, p), ALU.mult, "pool")
        tt(Mr[:], Mr[:], t1[:], ALU.add)
        tt(v3(Mi), v3(Xi), tabj(Ur, p), ALU.mult, "pool")
        tt(v3(t2), v3(Xr), tabj(Ui, p), ALU.mult)
        tt(Mi[:], Mi[:], t2[:], ALU.subtract, "pool")
        kb.ts("pool", d0[:], rmask[:], V(rho8.h[:, p:p + 1], rho8.name), ALU.mult)
        kb.scan(Xr[:], d0[:], Mr[:], 0.0, ALU.mult, ALU.add)
        kb.scan(Xi[:], d0[:], Mi[:], 0.0, ALU.mult, ALU.add)
        tt(v3(Mr), v3(Xr), tabj(Ur, p), ALU.mult)
        tt(v3(t1), v3(Xi), tabj(Ui, p), ALU.mult, "pool")
        tt(Mr[:], Mr[:], t1[:], ALU.subtract)
        tt(v3(Mi), v3(Xi), tabj(Ur, p), ALU.mult, "pool")
        tt(v3(t2), v3(Xr), tabj(Ui, p), ALU.mult)
        tt(Mi[:], Mi[:], t2[:], ALU.add, "pool")
        Ek_r = V(v3(Mr).ap[:, :, SCB - 1], Mr.name)
        Ek_i = V(v3(Mi).ap[:, :, SCB - 1], Mi.name)

        def tk(t):
            return V(t.h[:, p, 0:nsc], t.name)

        En = [V(x.h[:, 0:nsc], x.name) for x in (Er, Ei, e1, e2)]
        tt(En[0], Ek_r, tk(Wr), ALU.mult)
        tt(En[2], Ek_i, tk(Wi), ALU.mult)
        tt(En[0], En[0], En[2], ALU.add)
        tt(En[1], Ek_i, tk(Wr), ALU.mult)
        tt(En[3], Ek_r, tk(Wi), ALU.mult)
        tt(En[1], En[1], En[3], ALU.subtract)
        rho32 = V(Rj.h[:, p, NJ - 1:NJ], Rj.name)
        d0k = V(e1.h[:, 0:nsc], e1.name)
        kb.op("pool", lambda e: e.memset(e1.h[:, 0:nsc], 1.0), [], [e1[:]])
        kb.ts("dve", d0k, d0k, rho32, ALU.mult)
        kb.scan(En[0], d0k, En[0], 0.0, ALU.mult, ALU.add)
        kb.scan(En[1], d0k, En[1], 0.0, ALU.mult, ALU.add)
        kb.op("pool", lambda e: e.memset(Fr.h[:, 0:1], 0.0), [], [Fr[:]])
        kb.op("pool", lambda e: e.memset(Fi.h[:, 0:1], 0.0), [], [Fi[:]])
        Fr1, Fi1 = V(Fr.h[:, 1:nsc + 1], Fr.name), V(Fi.h[:, 1:nsc + 1], Fi.name)
        tt(Fr1, En[0], tk(Wr), ALU.mult)
        tt(En[2], En[1], tk(Wi), ALU.mult)
        tt(Fr1, Fr1, En[2], ALU.subtract)
        tt(Fi1, En[1], tk(Wr), ALU.mult)
        tt(En[3], En[0], tk(Wi), ALU.mult)
        tt(Fi1, Fi1, En[3], ALU.add)
        def fk(t):
            return V(t.h[:, 0:nsc].unsqueeze(2).to_broadcast([128, nsc, SCB]), t.name)

        tt(v3(t1), tabj(Zr, p), fk(Fr), ALU.mult)
        tt(v3(t2), tabj(Zi, p), fk(Fi), ALU.mult, "pool")
        tt(Mr[:], Mr[:], t1[:], ALU.add)
        tt(Mr[:], Mr[:], t2[:], ALU.subtract)
        tt(v3(t1), tabj(Zr, p), fk(Fi), ALU.mult)
        tt(v3(t2), tabj(Zi, p), fk(Fr), ALU.mult, "pool")
        tt(Mi[:], Mi[:], t1[:], ALU.add, "pool")
        tt(Mi[:], Mi[:], t2[:], ALU.add, "pool")
        kb.op("pool", lambda e: e.memset(Hr.h[:, 0:1], 0.0), [], [Hr[:]])
        kb.op("pool", lambda e: e.memset(Hi.h[:, 0:1], 0.0), [], [Hi[:]])
        kb.copy("act", Hr[:, 1:nb], Mr[:, 0:nb - 1])
        kb.copy("act", Hi[:, 1:nb], Mi[:, 0:nb - 1])
        for gg in range(2):
            g = 2 * p + gg
            rows = slice(64 * gg, 64 * gg + 64)
            for pc in range(npc):
                cs_ = slice(pc * 512, (pc + 1) * 512)
                bn = bank()
                kb.mm(Pb(bn), Tm[g][:], Vbf[gg][:, cs_], start=True, stop=False)
                kb.mm(Pb(bn), V(ChRe[p].h[rows].rearrange("p a b -> p (a b)"), ChRe[p].name), Hr[rows, cs_],
                      start=False, stop=False)
                kb.mm(Pb(bn), V(ChIm[p].h[rows].rearrange("p a b -> p (a b)"), ChIm[p].name), Hi[rows, cs_],
                      start=False, stop=True)
                yp = ypre[(pc + gg) % 2]
                g1 = gt[(pc + gg) % 2]
                kb.stt(yp[:], V32[gg][:, cs_], dsk[:, g:g + 1], Pb(bn), ALU.mult, ALU.add)
                kb.tt("pool", g1[:], yp[:], yp[:], ALU.mult)
                kb.ts("pool", g1[:], g1[:], 0.044715, ALU.mult, 1.0, ALU.add)
                kb.tt("pool", g1[:], g1[:], yp[:], ALU.mult)
                kb.act(g1[:], g1[:], AF.Sigmoid, scale=1.5957691216057308)
                kb.tt("dve", g1[:], g1[:], yp[:], ALU.mult)
                kb.dma("sp", f"st_y{(pc + gg) % 2}", yout[g, :, cs_], g1[:], is_output=True)
    kb.finish()
    return kb


def prep_s5(inp, hn1_tok, nb=SEQ // 8):
    ntok = nb * 8
    u = hn1_tok[:ntok].reshape(nb, 8, 64, 16)
    vin_all = np.ascontiguousarray(u.transpose(2, 1, 3, 0).reshape(64, 128, nb))
    msk = np.zeros((128, 2, 128), np.float32)
    tt_ = np.arange(128) // 16
    msk[:, 0, :] = (tt_[None, :] >= tt_[:, None]).astype(np.float32)
    msk[:, 1, :] = np.eye(128, dtype=np.float32)
    maps = []
    for c in range(NCORES):
        gs = np.arange(8 * c, 8 * c + 8).reshape(4, 2)
        lam = np.stack([inp["l1_lambda_re"][gs], inp["l1_lambda_im"][gs]], axis=-1)
        lam = np.ascontiguousarray(lam.transpose(1, 2, 0, 3).reshape(128, 4, 2))
        dl = np.ascontiguousarray(np.broadcast_to(inp["l1_log_step"][gs][:, :, None], (4, 2, 64)).transpose(1, 2, 0).reshape(128, 4))
        bri = np.stack([inp["l1_b_re"][gs], inp["l1_b_im"][gs]], axis=2)
        bri = np.ascontiguousarray(bri.transpose(1, 3, 0, 2, 4).reshape(128, 4, 2, 16))
        ctt = np.stack([inp["l1_c_re"][gs], inp["l1_c_im"][gs]], axis=2)
        ctt = np.ascontiguousarray(ctt.transpose(1, 4, 0, 2, 3).reshape(128, 4, 2, 16))
        dsk = inp["l1_d_skip"].reshape(64, 16)[8 * c:8 * c + 8]
        dsk = np.ascontiguousarray(np.broadcast_to(dsk.T[None, :, :], (8, 16, 8)).reshape(128, 8))
        maps.append(dict(vin=np.ascontiguousarray(vin_all[8 * c:8 * c + 8]), lam=lam, dl=dl.astype(np.float32),
                         bri=bri, ct=ctt, dsk=dsk, msk=msk))
    return maps


def unprep_s5(res, nb=SEQ // 8):
    y = np.concatenate([res[c]["yout"] for c in range(NCORES)], axis=0)
    y = y.reshape(64, 8, 16, nb).transpose(3, 1, 0, 2).reshape(nb * 8, 1024)
    return np.ascontiguousarray(y)


def _fm4_to_tok(res, name):
    return np.concatenate([res[c][name].transpose(2, 1, 0).reshape(TPC, -1) for c in range(NCORES)], axis=0)


def kernel(**inp):
    inp = {k: np.asarray(v) for k, v in inp.items()}
    res1 = _run(build_l1(), prep_l1(inp))
    res2 = _run(build_l2(), prep_l2(res1))
    yrec = np.concatenate([res2[h]["y"].transpose(1, 0, 2).reshape(SEQ, 64) for h in range(NCORES)], axis=1)
    extra = dict(yrec=yrec, gg=_fm4_to_tok(res1, "o_g"), bon=_fm4_to_tok(res1, "o_bonus"),
                 ybb=_fm4_to_tok(res1, "o_yb"))
    del res2
    res3 = _run(build_post(0), prep_post(0, inp, inp["x"][0], extra))
    del res1, extra
    h0 = np.concatenate([_unfm(res3[c]["h_out"]) for c in range(NCORES)], axis=0)
    hn1 = np.concatenate([_unfm(res3[c]["n_out"]) for c in range(NCORES)], axis=0)
    del res3
    res4 = _run(build_s5(), prep_s5(inp, hn1))
    yg = unprep_s5(res4)
    del res4
    res5 = _run(build_post(1), prep_post(1, inp, h0, dict(yg=yg)))
    out = np.concatenate([_unfm(res5[c]["n_out"]) for c in range(NCORES)], axis=0)
    return out.reshape(1, SEQ, D).astype(np.float32)
```
